# Optimizing a Trainium2 kernel written in Bass

```python
import jax, jax.numpy as jnp
from jax import lax
import numpy as np

D_MODEL = 2048
BATCH = 4
SEQ = 8192
DEPTH = 2

N_MIXERS = 2
MIX_WIDTH = D_MODEL
FOX_HEADS = 16
FOX_HEAD_DIM = MIX_WIDTH // FOX_HEADS
Q_BLOCK = 128
HGRN_HEADS = 16
HGRN_KEY_DIM = MIX_WIDTH // HGRN_HEADS
HGRN_VAL_DIM = MIX_WIDTH // HGRN_HEADS
HGRN_CHUNK = 64
N_FOX_LAYERS = (DEPTH + 1) // 2
N_HGRN_LAYERS = DEPTH // 2
FOX_IN = 4 * MIX_WIDTH + FOX_HEADS
HGRN_IN = 4 * MIX_WIDTH
EPS = 1e-6

kernel_name = "fox_hgrn2_interleaved_hybrid"


def rms_norm(x, gain):
    xf = x.astype(jnp.float32)
    y = xf * lax.rsqrt(jnp.mean(xf * xf, axis=-1, keepdims=True) + EPS)
    return y.astype(x.dtype) * gain


def split_heads(t, n_heads):
    b, s, _ = t.shape
    return t.reshape(b, s, n_heads, -1).transpose(0, 2, 1, 3)


def fox_mixer(h, w_in, b_f):
    B, S, _ = h.shape
    W, H, dh = MIX_WIDTH, FOX_HEADS, FOX_HEAD_DIM
    proj = h @ w_in
    q = split_heads(proj[..., :W], H)
    k = split_heads(proj[..., W:2 * W], H)
    v = split_heads(proj[..., 2 * W:3 * W], H)
    f_logit = proj[..., 3 * W:3 * W + H]
    gate = proj[..., 3 * W + H:]
    log_f = jax.nn.log_sigmoid((f_logit + b_f).astype(jnp.float32)).transpose(0, 2, 1)
    c = jnp.cumsum(log_f, axis=-1)
    nq = S // Q_BLOCK
    qb = q.reshape(B, H, nq, Q_BLOCK, dh).transpose(2, 0, 1, 3, 4)
    cb = c.reshape(B, H, nq, Q_BLOCK).transpose(2, 0, 1, 3)
    starts = jnp.arange(nq) * Q_BLOCK
    key_pos = jnp.arange(S)
    scale = dh ** -0.5

    def block(args):
        q_blk, c_blk, start = args
        logits = jnp.einsum('bhqd,bhkd->bhqk', q_blk, k).astype(jnp.float32) * scale
        logits = logits + (c_blk[..., :, None] - c[:, :, None, :])
        q_pos = start + jnp.arange(Q_BLOCK)
        causal = key_pos[None, :] <= q_pos[:, None]
        logits = jnp.where(causal, logits, -jnp.inf)
        p = jax.nn.softmax(logits, axis=-1).astype(v.dtype)
        return jnp.einsum('bhqk,bhkd->bhqd', p, v)

    o = lax.map(block, (qb, cb, starts))
    o = o.transpose(1, 0, 3, 2, 4).reshape(B, S, W)
    return o * jax.nn.silu(gate)


def hgrn2_mixer(h, w_in, lb, onorm_gain):
    B, S, _ = h.shape
    H, dk, dv, C = HGRN_HEADS, HGRN_KEY_DIM, HGRN_VAL_DIM, HGRN_CHUNK
    proj = h @ w_in
    q_raw, f_raw, i_raw, gate = jnp.split(proj, 4, axis=-1)
    q = split_heads(jax.nn.silu(q_raw), H).astype(jnp.float32)
    fz = split_heads(f_raw, H).astype(jnp.float32)
    v = split_heads(i_raw, H).astype(jnp.float32)
    lb_h = lb.astype(jnp.float32).reshape(H, 1, dk)
    log_f = jnp.log(lb_h + (1.0 - lb_h) * jax.nn.sigmoid(fz))
    k = (1.0 - lb_h) * jax.nn.sigmoid(-fz)
    nc = S // C

    def to_chunks(t):
        return t.reshape(B, H, nc, C, t.shape[-1]).transpose(2, 0, 1, 3, 4)

    tri = jnp.tril(jnp.ones((C, C), dtype=bool))

    def step(state, inp):
        q_c, k_c, lf_c, v_c = inp
        b = jnp.cumsum(lf_c, axis=-2)
        b_last = b[:, :, -1, :]
        inter = jnp.einsum('bhtd,bhde->bhte', q_c * jnp.exp(b), state)
        rel = jnp.where(tri[:, :, None], b[:, :, :, None, :] - b[:, :, None, :, :], -jnp.inf)
        A = jnp.einsum('bhtd,bhsd,bhtsd->bhts', q_c, k_c, jnp.exp(rel))
        intra = jnp.einsum('bhts,bhse->bhte', A, v_c)
        new_state = jnp.exp(b_last)[..., None] * state + jnp.einsum(
            'bhsd,bhse->bhde', k_c * jnp.exp(b_last[:, :, None, :] - b), v_c)
        return new_state, inter + intra

    state0 = jnp.zeros((B, H, dk, dv), jnp.float32)
    _, o = lax.scan(step, state0, (to_chunks(q), to_chunks(k), to_chunks(log_f), to_chunks(v)))
    o = o.transpose(1, 2, 0, 3, 4).reshape(B, H, S, dv)
    o = o * lax.rsqrt(jnp.mean(o * o, axis=-1, keepdims=True) + EPS)
    o = o.transpose(0, 2, 1, 3).reshape(B, S, MIX_WIDTH).astype(h.dtype) * onorm_gain
    return o * jax.nn.silu(gate)


def setup_inputs(seed: int = 0) -> dict:
    key = jax.random.key(seed)
    ks = jax.random.split(key, 10)
    f32 = jnp.float32
    x = jax.random.normal(ks[0], (BATCH, SEQ, D_MODEL), f32)
    norm_gains = 1.0 + 0.02 * jax.random.normal(ks[1], (DEPTH, D_MODEL), f32)
    fox_w_in = jax.random.normal(ks[2], (N_FOX_LAYERS, D_MODEL, FOX_IN), f32) * D_MODEL ** -0.5
    fox_b_f = 1.0 + 0.1 * jax.random.normal(ks[3], (N_FOX_LAYERS, FOX_HEADS), f32)
    hgrn_w_in = jax.random.normal(ks[4], (N_HGRN_LAYERS, D_MODEL, HGRN_IN), f32) * D_MODEL ** -0.5
    hgrn_lb_logits = 0.5 * jax.random.normal(ks[5], (DEPTH, MIX_WIDTH), f32)
    hgrn_onorm = 1.0 + 0.02 * jax.random.normal(ks[6], (N_HGRN_LAYERS, MIX_WIDTH), f32)
    w_out = jax.random.normal(ks[7], (DEPTH, MIX_WIDTH, D_MODEL), f32) * MIX_WIDTH ** -0.5
    final_gain = 1.0 + 0.02 * jax.random.normal(ks[8], (D_MODEL,), f32)
    return {"x": x, "norm_gains": norm_gains, "fox_w_in": fox_w_in, "fox_b_f": fox_b_f,
            "hgrn_w_in": hgrn_w_in, "hgrn_lb_logits": hgrn_lb_logits, "hgrn_onorm": hgrn_onorm,
            "w_out": w_out, "final_gain": final_gain}


def reference(x, norm_gains, fox_w_in, fox_b_f, hgrn_w_in, hgrn_lb_logits, hgrn_onorm, w_out, final_gain):
    lb_all = jnp.cumsum(jax.nn.softmax(hgrn_lb_logits.astype(jnp.float32), axis=0), axis=0)
    lb_all = lb_all - lb_all[0:1]
    for i in range(DEPTH):
        h = rms_norm(x, norm_gains[i])
        j = i // N_MIXERS
        if i % N_MIXERS == 0:
            y = fox_mixer(h, fox_w_in[j], fox_b_f[j])
        else:
            y = hgrn2_mixer(h, hgrn_w_in[j], lb_all[i], hgrn_onorm[j])
        x = x + y @ w_out[i]
    return rms_norm(x, final_gain)
```

```python
from contextlib import ExitStack

import numpy as np
import ml_dtypes

import concourse.bass as bass
import concourse.mybir as mybir
from concourse.bass_utils import run_bass_kernel_spmd

F32 = mybir.dt.float32
BF16 = mybir.dt.bfloat16
AF = mybir.ActivationFunctionType
ALU = mybir.AluOpType

D = 2048
S = 8192
B = 4
NH = 16
DH = 128
HPC = 8
WC = HPC * DH
EPS = 1e-6
NT = S // 128
NCH = D // 128


class Sem:
    def __init__(self, h):
        self.h = h
        self.v = 0


class Prog:
    ENG = ("pe", "act", "dve", "pool", "sp")

    def __init__(self, nc, stack):
        self.nc = nc
        self.stack = stack
        self.q = {k: [] for k in self.ENG}
        self._n = 0

    def sem(self, name):
        return Sem(self.stack.enter_context(self.nc.semaphore(name)))

    def sbuf(self, name, shape, dt):
        return self.stack.enter_context(self.nc.sbuf_tensor(name, list(shape), dt))

    def psum(self, name, shape, dt):
        return self.stack.enter_context(self.nc.psum_tensor(name, list(shape), dt))

    def op(self, eng, fn, waits=(), sig=None, inc=1):
        val = None
        if sig is not None:
            sig.v += inc
            val = sig.v
        w = [(s.h, int(v)) for (s, v) in waits if v is not None and v > 0]

        def run(e):
            for (h, v) in w:
                e.wait_ge(h, v)
            ins = fn(e)
            if sig is not None:
                ins.then_inc(sig.h, inc)

        self.q[eng].append(run)
        return val

    def wait(self, eng, waits):
        w = [(s.h, int(v)) for (s, v) in waits if v is not None and v > 0]

        def run(e):
            for (h, v) in w:
                e.wait_ge(h, v)

        self.q[eng].append(run)

    def emit(self):
        q = self.q
        with self.nc.Block() as block:
            @block.tensor
            def _(e):
                for f in q["pe"]:
                    f(e)

            @block.scalar
            def _(e):
                for f in q["act"]:
                    f(e)

            @block.vector
            def _(e):
                for f in q["dve"]:
                    f(e)

            @block.gpsimd
            def _(e):
                for f in q["pool"]:
                    f(e)

            @block.sync
            def _(e):
                for f in q["sp"]:
                    f(e)
        self.q = {k: [] for k in self.ENG}


def proj_pass(P, mode, x_src, gain_t, ident_d, w_list, outs, T=S,
              evac_funcs=None, wf=None, bf=None, consts=None, c_outs=None, tag=""):
    nc = P.nc
    ngrp = len(w_list)
    nt = T // 128
    nst = T // 512

    gt = P.sbuf(tag + "gt", [128, NCH], F32)
    idb = P.sbuf(tag + "idb", [128, 128], BF16)
    Wb = [P.sbuf(tag + f"Wb{k}", [128, NCH, WC], BF16) for k in range(ngrp)]
    wst = [P.sbuf(tag + f"wst{i}", [128, WC], F32) for i in range(2)]
    xt = [P.sbuf(tag + f"xt{i}", [128, D], F32) for i in range(2)]
    xs = [P.sbuf(tag + f"xs{i}", [128, D], BF16) for i in range(2)]
    junk = P.sbuf(tag + "junk", [128, D], BF16)
    ss = P.sbuf(tag + "ss", [128, nt], F32)
    rs = P.sbuf(tag + "rs", [128, nt], F32)
    rstd = P.sbuf(tag + "rstd", [128, nt], F32)
    hT = [P.sbuf(tag + f"hT{i}", [128, NCH, 512], BF16) for i in range(2)]
    obuf = [P.sbuf(tag + f"ob{i}", [128, 512], BF16) for i in range(4)]
    psT = [P.psum(tag + f"psT{i}", [128, 8, 128], BF16) for i in range(2)]
    psA = [P.psum(tag + f"psA{i}", [128, 512], F32) for i in range(6)]

    s_cld = P.sem(tag + "cld")
    s_wld = [P.sem(tag + f"wld{i}") for i in range(2)]
    s_wcv = P.sem(tag + "wcv")
    s_xld = [P.sem(tag + f"xld{i}") for i in range(2)]
    s_ss = P.sem(tag + "ss")
    s_rs = P.sem(tag + "rs")
    s_rc = P.sem(tag + "rc")
    s_xs = P.sem(tag + "xs")
    s_tp = P.sem(tag + "tp")
    s_hc = P.sem(tag + "hc")
    s_acc = P.sem(tag + "acc")
    s_evA = P.sem(tag + "evA")
    s_evD = P.sem(tag + "evD")
    s_ost = [P.sem(tag + f"ost{i}") for i in range(4)]

    has_f = wf is not None
    if has_f:
        wfst = P.sbuf(tag + "wfst", [128, NCH, HPC], F32)
        Wfb = P.sbuf(tag + "Wfb", [128, NCH, HPC], BF16)
        bfb = P.sbuf(tag + "bfb", [128, HPC], F32)
        zt = P.sbuf(tag + "zt", [128, HPC], F32)
        et = P.sbuf(tag + "et", [128, HPC], F32)
        Lt = P.sbuf(tag + "Lt", [128, nt, HPC], F32)
        NFr = nt * HPC
        Ut = P.sbuf(tag + "Ut", [128, 128], F32)
        Ot = P.sbuf(tag + "Ot", [128, 128], F32)
        Mt = P.sbuf(tag + "Mt", [128, 128], F32)
        tot = P.sbuf(tag + "tot", [128, NFr], F32)
        incl = P.sbuf(tag + "incl", [128, NFr], F32)
        excl = P.sbuf(tag + "excl", [128, NFr], F32)
        ctk = P.sbuf(tag + "ctk", [128, NFr], F32)
        crf = P.sbuf(tag + "crf", [128, NFr], F32)
        ones64 = P.sbuf(tag + "ones64", [128, nt], F32)
        s_cld2 = P.sem(tag + "cld2")
        for ct_, cd_ in zip((Ut, Ot, Mt), consts):
            P.op("sp", lambda e, ct_=ct_, cd_=cd_: e.dma_start(out=ct_[:], in_=cd_[:, :]), sig=s_cld2, inc=16)
        s_fe = P.sem(tag + "fe")
        s_fl = P.sem(tag + "fl")

    P.op("sp", lambda e: e.dma_start(out=gt[:], in_=gain_t[:, :]), sig=s_cld, inc=16)
    s_cldp = P.sem(tag + "cldp")
    P.op("pool", lambda e: e.dma_start(out=idb[:], in_=ident_d[:, :]), sig=s_cldp, inc=16)
    ncl = 1
    if has_f:
        P.op("sp", lambda e: e.dma_start(out=wfst[:], in_=wf.rearrange("(c p) h -> p c h", p=128)),
             sig=s_cld, inc=16)
        P.op("sp", lambda e: e.dma_start(out=bfb[:], in_=bf.partition_broadcast(128)), sig=s_cld, inc=16)
        ncl = 3
    CLD = 16 * ncl

    i = 0
    for k in range(ngrp):
        for c in range(NCH):
            b = i % 2
            P.op("sp", lambda e, k=k, c=c, b=b: e.dma_start(out=wst[b][:], in_=w_list[k][c * 128:(c + 1) * 128, :]),
                 waits=[(s_wcv, i - 1)], sig=s_wld[b], inc=16)
            P.op("dve", lambda e, k=k, c=c, b=b: e.tensor_scalar(out=Wb[k][:, c, :], in0=wst[b][:],
                                                                 scalar1=gt[:, c:c + 1], scalar2=None, op0=ALU.mult),
                 waits=[(s_wld[b], 16 * (i // 2 + 1)), (s_cld, CLD)], sig=s_wcv)
            i += 1
    if has_f:
        for c in range(NCH):
            P.op("dve", lambda e, c=c: e.tensor_scalar(out=Wfb[:, c, :], in0=wfst[:, c, :],
                                                       scalar1=gt[:, c:c + 1], scalar2=None, op0=ALU.mult),
                 waits=[(s_cld, CLD)], sig=s_wcv)
    WREADY = s_wcv.v
    mm_done = {}

    def front(n):
        st, u = divmod(n, 4)
        b = n % 2
        P.op("sp", lambda e: e.dma_start(out=xt[b][:], in_=x_src[n * 128:(n + 1) * 128, :]),
             waits=[(s_xs, n - 1)], sig=s_xld[b], inc=16)
        P.op("act", lambda e: e.activation(out=junk[:], in_=xt[b][:], func=AF.Square, accum_out=ss[:, n:n + 1]),
             waits=[(s_xld[b], 16 * (n // 2 + 1))], sig=s_ss)
        P.op("act", lambda e: e.activation(out=rs[:, n:n + 1], in_=ss[:, n:n + 1], func=AF.Sqrt,
                                           scale=1.0 / D, bias=EPS),
             waits=[(s_ss, n + 1)], sig=s_rs)
        P.op("dve", lambda e: e.reciprocal(out=rstd[:, n:n + 1], in_=rs[:, n:n + 1]),
             waits=[(s_rs, n + 1)], sig=s_rc)
        P.op("dve", lambda e: e.tensor_scalar(out=xs[b][:], in0=xt[b][:], scalar1=rstd[:, n:n + 1],
                                              scalar2=None, op0=ALU.mult),
             waits=[(s_rc, n + 1), (s_tp, 2 * (n - 1))], sig=s_xs)
        for half in range(2):
            idx = 2 * n + half
            for cc in range(8):
                c = half * 8 + cc
                last = cc == 7
                first = cc == 0
                P.op("pe", lambda e, c=c, cc=cc, half=half: e.transpose(out=psT[half][:, cc, :], in_=xs[b][:, c * 128:(c + 1) * 128],
                                                             identity=idb[:]),
                     waits=([(s_xs, n + 1), (s_hc, idx - 1), (s_cldp, 16)] if first else ()),
                     sig=(s_tp if last else None))
            P.op("dve", lambda e, half=half: e.tensor_copy(out=hT[st % 2][:, half * 8:(half + 1) * 8, u * 128:(u + 1) * 128],
                                                in_=psT[half][:]),
                 waits=[(s_tp, idx + 1), (s_acc, mm_done.get(st - 2, 0))], sig=s_hc)

    acc_i = [0]
    out_i = [0]
    acc_eng = {}

    def evac_wait_for_acc(a):
        p = a - 6
        if p < 0:
            return []
        return [acc_eng[p]]

    def do_acc(mm_list, evac, dst, st):
        a = acc_i[0]
        acc_i[0] += 1
        ps = psA[a % 6]
        nmm = len(mm_list)
        for j, (l, r) in enumerate(mm_list):
            w = []
            if j == 0:
                w = evac_wait_for_acc(a) + [(s_wcv, WREADY), (s_hc, 8 * (st + 1))]
            P.op("pe", lambda e, l=l, r=r, j=j: e.matmul(ps[:], lhsT=l, rhs=r, start=(j == 0), stop=(j == nmm - 1)),
                 waits=w, sig=(s_acc if j == nmm - 1 else None))
        accv = s_acc.v
        o = out_i[0]
        out_i[0] += 1
        ob = obuf[o % 4]
        ostw = [(s_ost[o % 4], 16 * (o // 4))]
        if evac == "silu":
            P.op("act", lambda e: e.activation(out=ob[:], in_=ps[:], func=AF.Silu),
                 waits=[(s_acc, accv)] + ostw, sig=s_evA)
            acc_eng[a] = (s_evA, s_evA.v)
        elif evac == "copyA":
            P.op("act", lambda e: e.activation(out=ob[:], in_=ps[:], func=AF.Copy),
                 waits=[(s_acc, accv)] + ostw, sig=s_evA)
            acc_eng[a] = (s_evA, s_evA.v)
        else:
            P.op("dve", lambda e: e.tensor_copy(out=ob[:], in_=ps[:]),
                 waits=[(s_acc, accv)] + ostw, sig=s_evD)
            acc_eng[a] = (s_evD, s_evD.v)
        P.op("pool", lambda e: e.dma_start(out=dst, in_=ob[:]),
             waits=[acc_eng[a]], sig=s_ost[o % 4], inc=16)

    def do_f(st, u):
        n = st * 4 + u
        a = acc_i[0]
        acc_i[0] += 1
        ps = psA[a % 6]
        for c in range(NCH):
            w = []
            if c == 0:
                w = evac_wait_for_acc(a) + [(s_wcv, WREADY), (s_hc, 8 * (st + 1))]
            P.op("pe", lambda e, c=c: e.matmul(ps[:, 0:HPC], lhsT=hT[st % 2][:, c, u * 128:(u + 1) * 128],
                                               rhs=Wfb[:, c, :], start=(c == 0), stop=(c == NCH - 1)),
                 waits=w, sig=(s_acc if c == NCH - 1 else None))
        accv = s_acc.v
        P.op("dve", lambda e: e.tensor_tensor(out=zt[:], in0=ps[:, 0:HPC], in1=bfb[:], op=ALU.add),
             waits=[(s_acc, accv), (s_fe, s_fe.v), (s_cld, CLD)], sig=s_evD)
        acc_eng[a] = (s_evD, s_evD.v)
        P.op("act", lambda e: e.activation(out=et[:], in_=zt[:], func=AF.Exp, scale=-1.0),
             waits=[(s_evD, s_evD.v), (s_fl, s_fl.v)], sig=s_fe)
        P.op("act", lambda e: e.activation(out=Lt[:, n, :], in_=et[:], func=AF.Ln, bias=1.0),
             waits=[(s_fe, s_fe.v)], sig=s_fl)

    def mms(st):
        hb = hT[st % 2]
        if mode == "fm":
            for k in range(ngrp):
                for hd in range(HPC):
                    lst = [(Wb[k][:, c, hd * 128:(hd + 1) * 128], hb[:, c, :]) for c in range(NCH)]
                    do_acc(lst, "copyA" if (hd % 2 == 0) else "copyD",
                           outs[k][hd, :, st * 512:(st + 1) * 512], st)
        else:
            for u in range(4):
                for k in range(ngrp):
                    for half in range(2):
                        lst = [(hb[:, c, u * 128:(u + 1) * 128], Wb[k][:, c, half * 512:(half + 1) * 512])
                               for c in range(NCH)]
                        do_acc(lst, evac_funcs[k],
                               outs[k][st * 512 + u * 128: st * 512 + (u + 1) * 128,
                                       half * 512:(half + 1) * 512], st)
                if has_f:
                    do_f(st, u)
        mm_done[st] = s_acc.v

    for u in range(4):
        front(u)
    for st in range(nst):
        if st + 1 < nst:
            for u in range(4):
                front((st + 1) * 4 + u)
        mms(st)

    if has_f:
        U_d, ones_d, m63_d = consts
        ctok_d, cref_d = c_outs
        NF = nt * HPC
        L2 = Lt[:].rearrange("p n h -> p (n h)")
        P.wait("pe", [(s_evA, s_evA.v), (s_evD, s_evD.v), (s_fl, s_fl.v), (s_cld2, 48)])
        for i3, ct in enumerate((Ut, Ot, Mt)):
            P.op("pe", lambda e, i3=i3, ct=ct: e.matmul(psA[i3][:, 0:NF], lhsT=ct[:], rhs=L2, start=True, stop=True),
                 sig=s_acc)
        accv = s_acc.v
        tot3 = tot[:].rearrange("p (n h) -> p n h", h=HPC)
        inc3 = incl[:].rearrange("p (n h) -> p n h", h=HPC)
        P.op("dve", lambda e: e.tensor_copy(out=tot[:], in_=psA[1][:, 0:NF]), waits=[(s_acc, accv)], sig=s_evD)
        P.op("dve", lambda e: e.memset(ones64[:], 1.0), sig=s_evD)
        for h in range(HPC):
            P.op("dve", lambda e, h=h: e.tensor_tensor_scan(out=inc3[:, :, h], data0=ones64[:], data1=tot3[:, :, h],
                                                            initial=0.0, op0=ALU.mult, op1=ALU.add),
                 waits=[(s_evD, s_evD.v)], sig=s_evD)
        P.op("dve", lambda e: e.tensor_tensor(out=excl[:], in0=incl[:], in1=tot[:], op=ALU.subtract),
             waits=[(s_evD, s_evD.v)], sig=s_evD)
        P.op("dve", lambda e: e.scalar_tensor_tensor(out=ctk[:], in0=psA[0][:, 0:NF], scalar=-1.0, in1=excl[:],
                                                     op0=ALU.mult, op1=ALU.subtract),
             waits=[(s_evD, s_evD.v)], sig=s_evD)
        P.op("dve", lambda e: e.scalar_tensor_tensor(out=crf[:], in0=psA[2][:, 0:NF], scalar=-1.0, in1=excl[:],
                                                     op0=ALU.mult, op1=ALU.subtract),
             waits=[(s_evD, s_evD.v)], sig=s_evD)
        P.op("pool", lambda e: e.dma_start(out=ctok_d[:, :], in_=ctk[:]), waits=[(s_evD, s_evD.v)],
             sig=s_ost[0], inc=16)
        P.op("pool", lambda e: e.dma_start(out=cref_d[:, :], in_=crf[:]), waits=[(s_evD, s_evD.v)],
             sig=s_ost[1], inc=16)

    P.wait("pool", [(s_ost[i4], s_ost[i4].v) for i4 in range(4)])


def bias_off(j):
    return 64 * j - (j * (j - 1)) // 2


def fox_attn(P, qT_d, kT_d, v_d, sg_d, ctok_d, cref_d, ident_d, maskneg_d, y_d, nheads=HPC, tag="B"):
    NG = NT
    NB = bias_off(NG)
    scale = float(DH) ** -0.5
    qTs = [P.sbuf(tag + f"qT{i}", [128, S], BF16) for i in range(2)]
    kTs = [P.sbuf(tag + f"kT{i}", [128, S], BF16) for i in range(2)]
    Va = [P.sbuf(tag + f"Va{i}", [128, NG, DH + 1], BF16) for i in range(2)]
    sgs = [P.sbuf(tag + f"sg{i}", [128, NG, DH], BF16) for i in range(2)]
    bias = [P.sbuf(tag + f"bias{i}", [128, NB], F32) for i in range(2)]
    ctk = P.sbuf(tag + "ctk", [128, NG * HPC], F32)
    crf = P.sbuf(tag + "crf", [128, NG * HPC], F32)
    idb = P.sbuf(tag + "idb", [128, 128], BF16)
    mneg = P.sbuf(tag + "mneg", [128, 128], BF16)
    pt = [P.sbuf(tag + f"pt{i}", [128, 512], BF16) for i in range(3)]
    yt = [P.sbuf(tag + f"yt{i}", [128, DH], BF16) for i in range(4)]
    rinv = P.sbuf(tag + "rinv", [128, 4], F32)
    psS = [P.psum(tag + f"psS{i}", [128, 512], F32) for i in range(3)]
    psOb = [P.psum(tag + f"psO{i}", [128, 512], F32) for i in range(5)]

    s_cld = P.sem(tag + "cld")
    s_cldp = P.sem(tag + "cldp")
    s_ld = [P.sem(tag + f"ld{i}") for i in range(2)]
    s_ones = P.sem(tag + "ones")
    s_bias = P.sem(tag + "bias")
    s_S = P.sem(tag + "S")
    s_P = P.sem(tag + "P")
    s_PV = P.sem(tag + "PV")
    s_rv = P.sem(tag + "rv")
    s_Oev = P.sem(tag + "Oev")
    s_yst = [P.sem(tag + f"yst{i}") for i in range(4)]

    ctk3 = ctk[:].rearrange("p (n h) -> p n h", h=HPC)
    crf3 = crf[:].rearrange("p (n h) -> p n h", h=HPC)

    P.op("sp", lambda e: e.dma_start(out=ctk[:], in_=ctok_d[:, :]), sig=s_cld, inc=16)
    P.op("sp", lambda e: e.dma_start(out=crf[:], in_=cref_d[:, :]), sig=s_cld, inc=16)
    P.op("pool", lambda e: e.dma_start(out=idb[:], in_=ident_d[:, :]), sig=s_cldp, inc=16)
    P.op("pool", lambda e: e.dma_start(out=mneg[:], in_=maskneg_d[:, :]), sig=s_cldp, inc=16)
    for i in range(2):
        P.op("pool", lambda e, i=i: e.memset(Va[i][:, :, DH:DH + 1], 1.0), sig=s_ones)

    steps = []
    for h in range(nheads):
        for qb in range(NG // 4):
            for j in range(4 * qb + 4):
                steps.append((h, qb, j))
    nsteps = len(steps)
    head_last_step = {}
    for i, (h, qb, j) in enumerate(steps):
        head_last_step[h] = i

    def loads(h):
        hb = h % 2
        w = []
        if h >= 2:
            w = [(s_PV, head_last_step[h - 2] + 1), (s_Oev, 64 * (h - 1))]
        P.op("sp", lambda e: e.dma_start(out=qTs[hb][:], in_=qT_d[h, :, :]), waits=w, sig=s_ld[hb], inc=16)
        P.op("sp", lambda e: e.dma_start(out=kTs[hb][:], in_=kT_d[h, :, :]), sig=s_ld[hb], inc=16)
        P.op("sp", lambda e: e.dma_start(out=Va[hb][:, :, 0:DH],
                                         in_=v_d[:, h * DH:(h + 1) * DH].rearrange("(j p) e -> p j e", p=128)),
             waits=[(s_ones, 2)], sig=s_ld[hb], inc=16)
        P.op("sp", lambda e: e.dma_start(out=sgs[hb][:],
                                         in_=sg_d[:, h * DH:(h + 1) * DH].rearrange("(j p) e -> p j e", p=128)),
             sig=s_ld[hb], inc=16)

    bias_ready = {}

    def build_bias(h):
        hb = h % 2
        for j in range(NG):
            w = []
            if j == 0:
                w = [(s_cld, 32)]
                if h >= 2:
                    w.append((s_P, head_last_step[h - 2] + 1))
            o = bias_off(j)
            P.op("pool", lambda e, j=j, o=o: e.tensor_scalar(out=bias[hb][:, o:o + NG - j], in0=crf3[:, j:NG, h],
                                                             scalar1=ctk3[:, j, h:h + 1], scalar2=None,
                                                             op0=ALU.subtract),
                 waits=w, sig=s_bias)
        bias_ready[h] = s_bias.v

    def active(qb, j):
        r = j - 4 * qb
        return list(range(max(r, 0), 4)), r

    def emit_S(i):
        h, qb, j = steps[i]
        hb = h % 2
        gls, r = active(qb, j)
        ps = psS[i % 3]
        kblk = kTs[hb][:, j * 128:(j + 1) * 128]
        w = [(s_P, i - 2)]
        if (qb, j) == (0, 0):
            w += [(s_ld[hb], 64 * (h // 2 + 1)), (s_cldp, 32)]
        if r < 0:
            P.op("pe", lambda e: e.matmul(ps[:, 0:512], lhsT=kblk, rhs=qTs[hb][:, qb * 512:(qb + 1) * 512],
                                          start=True, stop=True), waits=w, sig=s_S)
        else:
            q0 = (4 * qb + r) * 128
            has_rest = r < 3
            P.op("pe", lambda e: e.matmul(ps[:, r * 128:(r + 1) * 128], lhsT=kblk, rhs=qTs[hb][:, q0:q0 + 128],
                                          start=True, stop=False), waits=w)
            P.op("pe", lambda e: e.matmul(ps[:, r * 128:(r + 1) * 128], lhsT=idb[:], rhs=mneg[:],
                                          start=False, stop=True), sig=(None if has_rest else s_S))
            if has_rest:
                P.op("pe", lambda e: e.matmul(ps[:, (r + 1) * 128:512], lhsT=kblk,
                                              rhs=qTs[hb][:, q0 + 128:(qb + 1) * 512], start=True, stop=True),
                     sig=s_S)

    def emit_exp(i):
        h, qb, j = steps[i]
        hb = h % 2
        gls, r = active(qb, j)
        ps = psS[i % 3]
        p = pt[i % 3]
        for gl in gls:
            g = 4 * qb + gl
            col = bias_off(j) + (g - j)
            w = []
            if gl == gls[0]:
                w = [(s_S, i + 1), (s_PV, i - 2)]
                if (qb, j) == (0, 0):
                    w.append((s_bias, bias_ready[h]))
            P.op("act", lambda e, gl=gl, col=col: e.activation(out=p[:, gl * 128:(gl + 1) * 128],
                                                               in_=ps[:, gl * 128:(gl + 1) * 128], func=AF.Exp,
                                                               scale=scale, bias=bias[hb][:, col:col + 1]),
                 waits=w, sig=(s_P if gl == gls[-1] else None))

    def oacc(n):
        return psOb[n % 5][:, 0:DH + 1]

    def emit_PV(i):
        h, qb, j = steps[i]
        hb = h % 2
        gls, r = active(qb, j)
        p = pt[i % 3]
        Q = h * (NG // 4) + qb
        for gl in gls:
            g = 4 * qb + gl
            w = []
            if gl == gls[0]:
                w = [(s_P, i + 1)]
            if j == 0:
                w.append((s_Oev, h * NG + g - 4))
            oa = oacc(h * NG + g)
            P.op("pe", lambda e, gl=gl, oa=oa, g=g: e.matmul(oa, lhsT=p[:, gl * 128:(gl + 1) * 128],
                                                             rhs=Va[hb][:, j, :], start=(j == 0), stop=(j == g)),
                 waits=w, sig=(s_PV if gl == gls[-1] else None))
        if r >= 0:
            g = j
            gl = r
            n = h * NG + g
            oa = oacc(n)
            yb = yt[n % 4]
            P.op("dve", lambda e: e.reciprocal(out=rinv[:, n % 4:n % 4 + 1], in_=oa[:, DH:DH + 1]),
                 waits=[(s_PV, i + 1), (s_yst[n % 4], 16 * (n // 4))], sig=s_rv)
            P.op("dve", lambda e: e.scalar_tensor_tensor(out=yb[:], in0=oa[:, 0:DH], scalar=rinv[:, n % 4:n % 4 + 1],
                                                         in1=sgs[hb][:, g, :], op0=ALU.mult, op1=ALU.mult),
                 waits=[(s_rv, n + 1)], sig=s_Oev)
            P.op("pool", lambda e: e.dma_start(out=y_d[g * 128:(g + 1) * 128, h * DH:(h + 1) * DH], in_=yb[:]),
                 waits=[(s_Oev, n + 1)], sig=s_yst[n % 4], inc=16)

    loads(0)
    build_bias(0)
    cur_h = -1
    pending_S = 0

    def maybe_prefetch(i):
        pass

    authored_heads = {0}

    def ensure_head(h):
        if h < nheads and h not in authored_heads:
            loads(h)
            build_bias(h)
            authored_heads.add(h)

    for i in range(min(2, nsteps)):
        emit_S(i)
    for i in range(nsteps):
        h = steps[i][0]
        if steps[i][1:] == (0, 0):
            ensure_head(h + 1)
        emit_exp(i)
        emit_PV(i)
        if i + 2 < nsteps:
            emit_S(i + 2)
    P.wait("pool", [(s_yst[k], s_yst[k].v) for k in range(4)])


def wout_pass(P, y_d, x_d, w_d, rowgain_t, ident_d, out_d, T, final_gain_d=None, tag="C"):
    nt = T // 128
    final = final_gain_d is not None
    gt = P.sbuf(tag + "gt", [128, NCH], F32)
    idb = P.sbuf(tag + "idb", [128, 128], BF16)
    Wb = P.sbuf(tag + "Wb", [128, NCH, D], BF16)
    wst = [P.sbuf(tag + f"wst{i}", [128, D], F32) for i in range(2)]
    yt = [P.sbuf(tag + f"yt{i}", [128, D], BF16) for i in range(2)]
    xr = [P.sbuf(tag + f"xr{i}", [128, D], F32) for i in range(2)]
    xo = [P.sbuf(tag + f"xo{i}", [128, D], F32) for i in range(2)]
    yT = [P.sbuf(tag + f"yT{i}", [128, NCH, 128], BF16) for i in range(2)]
    psT = [P.psum(tag + f"psT{i}", [128, 8, 128], BF16) for i in range(2)]
    psA = [P.psum(tag + f"psA{i}", [128, 512], F32) for i in range(6)]
    s_cld = P.sem(tag + "cld")
    s_cldp = P.sem(tag + "cldp")
    s_wld = [P.sem(tag + f"wld{i}") for i in range(2)]
    s_wcv = P.sem(tag + "wcv")
    s_yld = [P.sem(tag + f"yld{i}") for i in range(2)]
    s_xld = [P.sem(tag + f"xld{i}") for i in range(2)]
    s_tp = P.sem(tag + "tp")
    s_hc = P.sem(tag + "hc")
    s_acc = P.sem(tag + "acc")
    s_ev = P.sem(tag + "ev")
    s_ost = [P.sem(tag + f"ost{i}") for i in range(2)]
    if final:
        fgb = P.sbuf(tag + "fgb", [128, D], F32)
        fo = [P.sbuf(tag + f"fo{i}", [128, D], F32) for i in range(2)]
        junk = P.sbuf(tag + "junk", [128, D], BF16)
        ss = P.sbuf(tag + "ss", [128, nt], F32)
        rs = P.sbuf(tag + "rs", [128, nt], F32)
        rstd = P.sbuf(tag + "rstd", [128, nt], F32)
        s_ss = P.sem(tag + "ss")
        s_rs = P.sem(tag + "rs")
        s_rc = P.sem(tag + "rc")
        s_fo = P.sem(tag + "fo")

    P.op("sp", lambda e: e.dma_start(out=gt[:], in_=rowgain_t[:, :]), sig=s_cld, inc=16)
    CLD = 16
    if final:
        P.op("sp", lambda e: e.dma_start(out=fgb[:], in_=final_gain_d.partition_broadcast(128)), sig=s_cld, inc=16)
        CLD = 32
    P.op("pool", lambda e: e.dma_start(out=idb[:], in_=ident_d[:, :]), sig=s_cldp, inc=16)
    for c in range(NCH):
        b = c % 2
        P.op("sp", lambda e, c=c, b=b: e.dma_start(out=wst[b][:], in_=w_d[c * 128:(c + 1) * 128, :]),
             waits=[(s_wcv, c - 1)], sig=s_wld[b], inc=16)
        P.op("dve", lambda e, c=c, b=b: e.tensor_scalar(out=Wb[:, c, :], in0=wst[b][:], scalar1=gt[:, c:c + 1],
                                                        scalar2=None, op0=ALU.mult),
             waits=[(s_wld[b], 16 * (c // 2 + 1)), (s_cld, CLD)], sig=s_wcv)
    WREADY = s_wcv.v
    ev_done = {}
    st_done = {}

    def front(n):
        b = n % 2
        P.op("sp", lambda e: e.dma_start(out=yt[b][:], in_=y_d[n * 128:(n + 1) * 128, :]),
             waits=[(s_tp, 2 * (n - 1))], sig=s_yld[b], inc=16)
        P.op("sp", lambda e: e.dma_start(out=xr[b][:], in_=x_d[n * 128:(n + 1) * 128, :]),
             waits=[(s_ev, ev_done.get(n - 2, 0))], sig=s_xld[b], inc=16)
        for half in range(2):
            idx = 2 * n + half
            for cc in range(8):
                c = half * 8 + cc
                P.op("pe", lambda e, c=c, cc=cc, half=half: e.transpose(out=psT[half][:, cc, :],
                                                                       in_=yt[b][:, c * 128:(c + 1) * 128],
                                                                       identity=idb[:]),
                     waits=([(s_yld[b], 16 * (n // 2 + 1)), (s_hc, idx - 1), (s_cldp, 16)] if cc == 0 else ()),
                     sig=(s_tp if cc == 7 else None))
            P.op("act", lambda e, half=half: e.activation(out=yT[b][:, half * 8:(half + 1) * 8, :], in_=psT[half][:],
                                                         func=AF.Copy),
                 waits=[(s_tp, idx + 1), (s_acc, 4 * (n - 1))], sig=s_hc)

    def mms(n):
        b = n % 2
        for q in range(4):
            a = 4 * n + q
            ps = psA[a % 6]
            for c in range(NCH):
                w = []
                if c == 0:
                    w = [(s_ev, a - 5), (s_wcv, WREADY), (s_hc, 2 * (n + 1))]
                P.op("pe", lambda e, c=c, q=q, ps=ps: e.matmul(ps[:], lhsT=yT[b][:, c, :], rhs=Wb[:, c, q * 512:(q + 1) * 512],
                                                        start=(c == 0), stop=(c == NCH - 1)),
                     waits=w, sig=(s_acc if c == NCH - 1 else None))
            w = [(s_acc, a + 1), (s_xld[b], 16 * (n // 2 + 1))]
            if q == 0:
                if final:
                    w.append((s_fo, n - 1))
                else:
                    w.append((s_ost[b], 16 * (n // 2)))
            P.op("dve", lambda e, q=q, ps=ps: e.tensor_tensor(out=xo[b][:, q * 512:(q + 1) * 512], in0=ps[:],
                                                       in1=xr[b][:, q * 512:(q + 1) * 512], op=ALU.add),
                 waits=w, sig=s_ev)
        ev_done[n] = s_ev.v
        if not final:
            P.op("pool", lambda e: e.dma_start(out=out_d[n * 128:(n + 1) * 128, :], in_=xo[b][:]),
                 waits=[(s_ev, ev_done[n])], sig=s_ost[b], inc=16)
        else:
            P.op("act", lambda e: e.activation(out=junk[:], in_=xo[b][:], func=AF.Square, accum_out=ss[:, n:n + 1]),
                 waits=[(s_ev, ev_done[n])], sig=s_ss)
            P.op("act", lambda e: e.activation(out=rs[:, n:n + 1], in_=ss[:, n:n + 1], func=AF.Sqrt,
                                               scale=1.0 / D, bias=EPS),
                 waits=[(s_ss, n + 1)], sig=s_rs)
            P.op("dve", lambda e: e.reciprocal(out=rstd[:, n:n + 1], in_=rs[:, n:n + 1]),
                 waits=[(s_rs, n + 1)], sig=s_rc)
            P.op("dve", lambda e: e.scalar_tensor_tensor(out=fo[b][:], in0=xo[b][:], scalar=rstd[:, n:n + 1],
                                                         in1=fgb[:], op0=ALU.mult, op1=ALU.mult),
                 waits=[(s_rc, n + 1), (s_ost[b], 16 * (n // 2)), (s_cld, CLD)], sig=s_fo)
            P.op("pool", lambda e: e.dma_start(out=out_d[n * 128:(n + 1) * 128, :], in_=fo[b][:]),
                 waits=[(s_fo, n + 1)], sig=s_ost[b], inc=16)

    front(0)
    for n in range(nt):
        if n + 1 < nt:
            front(n + 1)
        mms(n)
    P.wait("pool", [(s_ost[k], s_ost[k].v) for k in range(2)])


class Ctr:
    def __init__(self, P, tag):
        self.P = P
        self.c = {k: P.sem(tag + "c_" + k) for k in ("pe", "act", "dve", "pool")}

    @staticmethod
    def _w(deps):
        best = {}
        for d in deps:
            if d is None:
                continue
            s, v = d
            if v is None or v <= 0:
                continue
            if id(s) not in best or best[id(s)][1] < v:
                best[id(s)] = (s, v)
        return list(best.values())

    def do(self, eng, fn, deps=()):
        v = self.P.op(eng, fn, waits=self._w(deps), sig=self.c[eng])
        return (self.c[eng], v)

    def dma(self, queue, fn, sem, deps=()):
        v = self.P.op(queue, fn, waits=self._w(deps), sig=sem, inc=16)
        return (sem, v)

    def now(self):
        return [(s, s.v) for s in self.c.values()]


def hgrn_phase(P, qrT_d, frT_d, v_d, sg_d, lbl_d, ident_d, cmask_d, segm_d, y_d, nblk=S // 512, nheads=HPC,
               tag="H"):
    C = 64
    NCK = 8
    TB = 512
    K = Ctr(P, tag)
    idb = P.sbuf(tag + "idb", [128, 128], BF16)
    cm = P.sbuf(tag + "cm", [64, NCK, 64], F32)
    segm = P.sbuf(tag + "segm", [128, TB], F32)
    lbl = P.sbuf(tag + "lbl", [128, 2 * HPC], F32)
    lbd = P.sbuf(tag + "lbd", [128, HPC], F32)
    lbe = P.sbuf(tag + "lbe", [128, HPC], F32)
    lb = P.sbuf(tag + "lb", [128, HPC], F32)
    oml = P.sbuf(tag + "oml", [128, HPC], F32)
    lnoml = P.sbuf(tag + "lnoml", [128, HPC], F32)
    St = [P.sbuf(tag + f"St{h}", [128, 128], F32) for h in range(nheads)]
    Sb = [P.sbuf(tag + f"Sb{h}", [128, 128], BF16) for h in range(nheads)]

    def two(name, shape, dt):
        return [P.sbuf(tag + f"{name}{i}", shape, dt) for i in range(2)]

    qr = two("qr", [128, TB], BF16)
    fr = two("fr", [128, TB], BF16)
    vt = two("vt", [64, NCK, 128], BF16)
    sgt = two("sgt", [64, NCK, 128], BF16)
    e_ = two("e", [128, TB], F32)
    l1 = two("l1", [128, TB], F32)
    lf = two("lf", [128, TB], F32)
    bb = two("bb", [128, TB], F32)
    lnk = two("lnk", [128, TB], F32)
    lq = two("lq", [128, TB], F32)
    a1 = two("a1", [128, TB], F32)
    a2 = two("a2", [128, TB], F32)
    a3 = two("a3", [128, TB], F32)
    qt = two("qt", [128, TB], BF16)
    kh = two("kh", [128, TB], BF16)
    kbT = two("kbT", [128, TB], BF16)
    kb = two("kb", [64, NCK, 128], BF16)
    AT = two("AT", [64, NCK, 64], BF16)
    ebl = two("ebl", [128, NCK], F32)
    yt = two("yt", [64, NCK, 128], BF16)
    ss = two("ss", [64, NCK], F32)
    lnr = two("lnr", [64, NCK], F32)
    rstd = two("rstd", [64, NCK], F32)
    junk = P.sbuf(tag + "junk", [64, 128], BF16)
    psK = [P.psum(tag + f"psK{i}", [64, NCK, 128], BF16) for i in range(2)]
    psAT = [P.psum(tag + f"psAT{i}", [64, NCK, 64], F32) for i in range(2)]
    psO = [P.psum(tag + f"psO{i}", [64, 128], F32) for i in range(2)]
    psD = [P.psum(tag + f"psD{i}", [128, 128], F32) for i in range(2)]

    s_cld = P.sem(tag + "cld")
    s_cldp = P.sem(tag + "cldp")
    s_ld = [P.sem(tag + f"ld{i}") for i in range(2)]
    s_yst = [P.sem(tag + f"yst{i}") for i in range(2)]

    t_c = [K.dma("sp", lambda e: e.dma_start(out=segm[:], in_=segm_d[:, :]), s_cld),
           K.dma("sp", lambda e: e.dma_start(out=lbl[:], in_=lbl_d[:, :]), s_cld)]
    for c in range(NCK):
        t_c.append(K.dma("sp", lambda e, c=c: e.dma_start(out=cm[:, c, :], in_=cmask_d[:, :]), s_cld))
    t_cld = (s_cld, s_cld.v)
    t_id = K.dma("pool", lambda e: e.dma_start(out=idb[:], in_=ident_d[:, :]), s_cldp)

    t = K.do("dve", lambda e: e.tensor_tensor(out=lbd[:], in0=lbl[:, 0:HPC], in1=lbl[:, HPC:2 * HPC], op=ALU.subtract),
             [t_cld])
    t = K.do("act", lambda e: e.activation(out=lbe[:], in_=lbd[:], func=AF.Exp), [t])
    t = K.do("dve", lambda e: e.tensor_scalar(out=lbe[:], in0=lbe[:], scalar1=1.0, scalar2=None, op0=ALU.add), [t])
    t = K.do("dve", lambda e: e.reciprocal(out=lb[:], in_=lbe[:]), [t])
    t = K.do("dve", lambda e: e.tensor_scalar(out=oml[:], in0=lb[:], scalar1=-1.0, scalar2=1.0, op0=ALU.mult,
                                              op1=ALU.add), [t])
    t_lb = K.do("act", lambda e: e.activation(out=lnoml[:], in_=oml[:], func=AF.Ln), [t])
    t_st = []
    for h in range(nheads):
        t_st.append(K.do("pool", lambda e, h=h: e.memset(St[h][:], 0.0)))
        t_st.append(K.do("pool", lambda e, h=h: e.memset(Sb[h][:], 0.0)))
    st_tok = {h: t_st[-1] for h in range(nheads)}
    sb_tok = {h: t_st[-1] for h in range(nheads)}
    sb_read = {h: None for h in range(nheads)}

    end_tok = {}
    ckc = [0]
    o_free = {}
    d_free = {}

    def head_block(it, blk, h):
        s = it % 2
        prev = end_tok.get(it - 2, [])
        t0 = blk * TB
        ld = [
            K.dma("sp", lambda e: e.dma_start(out=qr[s][:], in_=qrT_d[h, :, t0:t0 + TB]), s_ld[s], prev),
            K.dma("sp", lambda e: e.dma_start(out=fr[s][:], in_=frT_d[h, :, t0:t0 + TB]), s_ld[s]),
            K.dma("sp", lambda e: e.dma_start(out=vt[s][:], in_=v_d[t0:t0 + TB, h * 128:(h + 1) * 128]
                                              .rearrange("(c s) e -> s c e", s=C)), s_ld[s]),
            K.dma("sp", lambda e: e.dma_start(out=sgt[s][:], in_=sg_d[t0:t0 + TB, h * 128:(h + 1) * 128]
                                              .rearrange("(c s) e -> s c e", s=C)), s_ld[s]),
        ]
        t_ld = (s_ld[s], s_ld[s].v)
        lbh = lb[:, h:h + 1]
        lno = lnoml[:, h:h + 1]
        t_e = K.do("act", lambda e: e.activation(out=e_[s][:], in_=fr[s][:], func=AF.Exp, scale=-1.0), [t_ld, t_lb])
        t_l1 = K.do("act", lambda e: e.activation(out=l1[s][:], in_=e_[s][:], func=AF.Ln, bias=1.0), [t_e])
        t_l2 = K.do("act", lambda e: e.activation(out=lf[s][:], in_=e_[s][:], func=AF.Ln, scale=lbh, bias=1.0),
                    [t_e])
        t_eq = K.do("act", lambda e: e.activation(out=lq[s][:], in_=qr[s][:], func=AF.Exp, scale=-1.0), [t_ld])
        t_lq = K.do("act", lambda e: e.activation(out=lq[s][:], in_=lq[s][:], func=AF.Ln, bias=1.0), [t_eq])
        t_lf = K.do("dve", lambda e: e.tensor_tensor(out=lf[s][:], in0=lf[s][:], in1=l1[s][:], op=ALU.subtract),
                    [t_l1, t_l2])
        t_b = K.do("dve", lambda e: e.tensor_tensor_scan(out=bb[s][:], data0=segm[:], data1=lf[s][:], initial=0.0,
                                                         op0=ALU.mult, op1=ALU.add), [t_lf, t_cld])
        t_lnk = K.do("dve", lambda e: e.scalar_tensor_tensor(out=lnk[s][:], in0=fr[s][:], scalar=-1.0, in1=l1[s][:],
                                                             op0=ALU.mult, op1=ALU.subtract), [t_l1])
        t_a1 = K.do("pool", lambda e: e.tensor_tensor(out=a1[s][:], in0=bb[s][:], in1=lq[s][:], op=ALU.subtract),
                    [t_b, t_lq])
        t_a2 = K.do("pool", lambda e: e.tensor_tensor(out=a2[s][:], in0=lnk[s][:], in1=bb[s][:], op=ALU.subtract),
                    [t_b, t_lnk])
        b3 = bb[s][:].rearrange("p (c s) -> p c s", s=C)
        a23 = a2[s][:].rearrange("p (c s) -> p c s", s=C)
        a33 = a3[s][:].rearrange("p (c s) -> p c s", s=C)
        t_a3 = K.do("pool", lambda e: e.tensor_tensor(out=a33, in0=a23,
                                                      in1=b3[:, :, C - 1:C].to_broadcast([128, NCK, C]),
                                                      op=ALU.add), [t_a2])
        t_x1 = K.do("act", lambda e: e.activation(out=a1[s][:], in_=a1[s][:], func=AF.Exp), [t_a1])
        t_qt = K.do("dve", lambda e: e.tensor_tensor(out=qt[s][:], in0=a1[s][:], in1=qr[s][:], op=ALU.mult), [t_x1])
        t_kh = K.do("act", lambda e: e.activation(out=kh[s][:], in_=a2[s][:], func=AF.Exp, bias=lno), [t_a2, t_a3])
        t_kb = K.do("act", lambda e: e.activation(out=kbT[s][:], in_=a3[s][:], func=AF.Exp, bias=lno), [t_a3])
        t_ebl = K.do("act", lambda e: e.activation(out=ebl[s][:], in_=b3[:, :, C - 1], func=AF.Exp), [t_b])
        tp = None
        for c in range(NCK):
            tp = K.do("pe", lambda e, c=c: e.transpose(out=psK[s][:, c, :], in_=kbT[s][:, c * C:(c + 1) * C],
                                                      identity=idb[:]), [t_kb, t_id] + prev)
        t_kbc = K.do("dve", lambda e: e.tensor_copy(out=kb[s][:], in_=psK[s][:]), [tp])
        ta = None
        for c in range(NCK):
            ta = K.do("pe", lambda e, c=c: e.matmul(psAT[s][:, c, :], lhsT=kh[s][:, c * C:(c + 1) * C],
                                                   rhs=qt[s][:, c * C:(c + 1) * C], start=True, stop=True),
                      [t_kh, t_qt])
        t_at = K.do("dve", lambda e: e.tensor_tensor(out=AT[s][:], in0=psAT[s][:], in1=cm[:], op=ALU.mult),
                    [ta, t_cld])
        t_y = []
        for c in range(NCK):
            k = ckc[0]
            ckc[0] += 1
            po = psO[k % 2]
            pd = psD[k % 2]
            K.do("pe", lambda e, c=c, po=po: e.matmul(po[:], lhsT=AT[s][:, c, :], rhs=vt[s][:, c, :],
                                                     start=True, stop=False), [t_at, t_ld, o_free.get(k - 2)])
            t_o = K.do("pe", lambda e, c=c, po=po: e.matmul(po[:], lhsT=qt[s][:, c * C:(c + 1) * C], rhs=Sb[h][:],
                                                           start=False, stop=True), [sb_tok[h], t_qt])
            sb_read[h] = t_o
            t_d = K.do("pe", lambda e, c=c, pd=pd: e.matmul(pd[:], lhsT=kb[s][:, c, :], rhs=vt[s][:, c, :],
                                                           start=True, stop=True), [t_kbc, d_free.get(k - 2)])
            t_s = K.do("dve", lambda e, c=c, pd=pd: e.scalar_tensor_tensor(out=St[h][:], in0=St[h][:],
                                                                          scalar=ebl[s][:, c:c + 1], in1=pd[:],
                                                                          op0=ALU.mult, op1=ALU.add),
                       [t_d, t_ebl, st_tok[h], sb_tok[h]])
            st_tok[h] = t_s
            d_free[k] = t_s
            t_sb = K.do("pool", lambda e: e.tensor_copy(out=Sb[h][:], in_=St[h][:]), [t_s, sb_read[h]])
            sb_tok[h] = t_sb
            t_ss = K.do("act", lambda e, c=c, po=po: e.activation(out=junk[:], in_=po[:], func=AF.Square,
                                                                 accum_out=ss[s][:, c:c + 1]), [t_o])
            t_ln = K.do("act", lambda e, c=c: e.activation(out=lnr[s][:, c:c + 1], in_=ss[s][:, c:c + 1],
                                                           func=AF.Ln, scale=1.0 / DH, bias=EPS), [t_ss])
            t_rs = K.do("act", lambda e, c=c: e.activation(out=rstd[s][:, c:c + 1], in_=lnr[s][:, c:c + 1],
                                                           func=AF.Exp, scale=-0.5), [t_ln])
            t_ev = K.do("dve", lambda e, c=c, po=po: e.scalar_tensor_tensor(out=yt[s][:, c, :], in0=po[:],
                                                                           scalar=rstd[s][:, c:c + 1],
                                                                           in1=sgt[s][:, c, :], op0=ALU.mult,
                                                                           op1=ALU.mult),
                        [t_rs, t_ld, (s_yst[s], 16 * (it // 2))])
            o_free[k] = t_ev
            t_y.append(t_ev)
        K.dma("pool", lambda e: e.dma_start(out=y_d[t0:t0 + TB, h * 128:(h + 1) * 128]
                                            .rearrange("(c s) e -> s c e", s=C), in_=yt[s][:]),
              s_yst[s], [t_y[-1]])
        end_tok[it] = K.now()

    it = 0
    for blk in range(nblk):
        for h in range(nheads):
            head_block(it, blk, h)
            it += 1
    P.wait("pool", [(s_yst[k], s_yst[k].v) for k in range(2)])


def _t16(vec):
    return np.ascontiguousarray(np.asarray(vec, np.float32).reshape(NCH, 128).T)


def _consts():
    ident = np.eye(128, dtype=np.float32)
    U = np.triu(np.ones((128, 128), np.float32))
    O = np.ones((128, 128), np.float32)
    M = np.tile((np.arange(128) <= 63).astype(np.float32)[:, None], (1, 128))
    mneg = np.where(np.arange(128)[:, None] > np.arange(128)[None, :], -30000.0, 0.0).astype(np.float32)
    cmask = (np.arange(64)[:, None] <= np.arange(64)[None, :]).astype(np.float32)
    segm = np.ones((128, 512), np.float32)
    segm[:, ::64] = 0
    return dict(ident=ident, U=U, O=O, M=M, mneg=mneg, cmask=cmask, segm=segm)


def _new_nc():
    return bass.Bass("TRN2", target_bir_lowering=False)


def _din(nc, name, shape, dt):
    return nc.dram_tensor(name, list(shape), dt, kind="ExternalInput").ap()


def _dout(nc, name, shape, dt):
    return nc.dram_tensor(name, list(shape), dt, kind="ExternalOutput").ap()


def build_proj_fm():
    nc = _new_nc()
    x = _din(nc, "x", [S, D], F32)
    gain_t = _din(nc, "gain_t", [128, NCH], F32)
    ident = _din(nc, "ident", [128, 128], F32)
    w0 = _din(nc, "w0", [D, WC], F32)
    w1 = _din(nc, "w1", [D, WC], F32)
    o0 = _dout(nc, "o0", [HPC, 128, S], BF16)
    o1 = _dout(nc, "o1", [HPC, 128, S], BF16)
    with ExitStack() as stack:
        P = Prog(nc, stack)
        proj_pass(P, "fm", x, gain_t, ident, [w0, w1], [o0, o1])
        P.emit()
    return nc


def build_proj_tm(with_f):
    nc = _new_nc()
    x = _din(nc, "x", [S, D], F32)
    gain_t = _din(nc, "gain_t", [128, NCH], F32)
    ident = _din(nc, "ident", [128, 128], F32)
    w0 = _din(nc, "w0", [D, WC], F32)
    w1 = _din(nc, "w1", [D, WC], F32)
    o0 = _dout(nc, "o0", [S, WC], BF16)
    o1 = _dout(nc, "o1", [S, WC], BF16)
    kw = {}
    if with_f:
        wf = _din(nc, "wf", [D, HPC], F32)
        bf = _din(nc, "bf", [HPC], F32)
        U = _din(nc, "U", [128, 128], F32)
        O = _din(nc, "O", [128, 128], F32)
        M = _din(nc, "M", [128, 128], F32)
        ctok = _dout(nc, "ctok", [128, NT * HPC], F32)
        cref = _dout(nc, "cref", [128, NT * HPC], F32)
        kw = dict(wf=wf, bf=bf, consts=(U, O, M), c_outs=(ctok, cref))
    with ExitStack() as stack:
        P = Prog(nc, stack)
        proj_pass(P, "tm", x, gain_t, ident, [w0, w1], [o0, o1], evac_funcs=["copyD", "silu"], **kw)
        P.emit()
    return nc


def build_attn():
    nc = _new_nc()
    qT = _din(nc, "qT", [HPC, 128, S], BF16)
    kT = _din(nc, "kT", [HPC, 128, S], BF16)
    v = _din(nc, "v", [S, WC], BF16)
    sg = _din(nc, "sg", [S, WC], BF16)
    ctok = _din(nc, "ctok", [128, NT * HPC], F32)
    cref = _din(nc, "cref", [128, NT * HPC], F32)
    ident = _din(nc, "ident", [128, 128], F32)
    mneg = _din(nc, "mneg", [128, 128], F32)
    y = _dout(nc, "y", [S, WC], BF16)
    with ExitStack() as stack:
        P = Prog(nc, stack)
        fox_attn(P, qT, kT, v, sg, ctok, cref, ident, mneg, y)
        P.emit()
    return nc


def build_hgrn():
    nc = _new_nc()
    qrT = _din(nc, "qrT", [HPC, 128, S], BF16)
    frT = _din(nc, "frT", [HPC, 128, S], BF16)
    v = _din(nc, "v", [S, WC], BF16)
    sg = _din(nc, "sg", [S, WC], BF16)
    lbl = _din(nc, "lbl", [128, 2 * HPC], F32)
    ident = _din(nc, "ident", [128, 128], F32)
    cmask = _din(nc, "cmask", [64, 64], F32)
    segm = _din(nc, "segm", [128, 512], F32)
    y = _dout(nc, "y", [S, WC], BF16)
    with ExitStack() as stack:
        P = Prog(nc, stack)
        hgrn_phase(P, qrT, frT, v, sg, lbl, ident, cmask, segm, y)
        P.emit()
    return nc


def build_wout(final):
    nc = _new_nc()
    T = S // 2
    y = _din(nc, "y", [T, D], BF16)
    x = _din(nc, "x", [T, D], F32)
    w = _din(nc, "w", [D, D], F32)
    rg = _din(nc, "rg", [128, NCH], F32)
    ident = _din(nc, "ident", [128, 128], F32)
    fg = _din(nc, "fg", [D], F32) if final else None
    out = _dout(nc, "out", [T, D], F32)
    with ExitStack() as stack:
        P = Prog(nc, stack)
        wout_pass(P, y, x, w, rg, ident, out, T, final_gain_d=fg)
        P.emit()
    return nc


def _run(nc, in_maps):
    res = run_bass_kernel_spmd(nc, in_maps, core_ids=list(range(len(in_maps))))
    return res.results


def kernel(x, norm_gains, fox_w_in, fox_b_f, hgrn_w_in, hgrn_lb_logits, hgrn_onorm, w_out, final_gain):
    x = np.asarray(x, np.float32)
    norm_gains = np.asarray(norm_gains, np.float32)
    fox_w = np.asarray(fox_w_in, np.float32)[0]
    fox_bf = np.asarray(fox_b_f, np.float32)[0]
    hg_w = np.asarray(hgrn_w_in, np.float32)[0]
    lbl = np.asarray(hgrn_lb_logits, np.float32)
    onorm = np.asarray(hgrn_onorm, np.float32)[0]
    w_out = np.asarray(w_out, np.float32)
    final_gain = np.asarray(final_gain, np.float32)
    cs = _consts()
    W = D
    cores = [(b, g) for b in range(B) for g in range(2)]
    H2 = S // 2

    def cs_(g, base):
        return np.ascontiguousarray

    g0 = _t16(norm_gains[0])

    def wcol(w, base, g):
        return np.ascontiguousarray(w[:, base + g * WC: base + (g + 1) * WC])

    r1 = _run(build_proj_fm(), [dict(x=x[b], gain_t=g0, ident=cs["ident"], w0=wcol(fox_w, 0, g),
                                     w1=wcol(fox_w, W, g)) for (b, g) in cores])
    r2 = _run(build_proj_tm(True), [dict(x=x[b], gain_t=g0, ident=cs["ident"], w0=wcol(fox_w, 2 * W, g),
                                         w1=wcol(fox_w, 3 * W + NH, g),
                                         wf=np.ascontiguousarray(fox_w[:, 3 * W + g * HPC: 3 * W + (g + 1) * HPC]),
                                         bf=np.ascontiguousarray(fox_bf[g * HPC:(g + 1) * HPC]),
                                         U=cs["U"], O=cs["O"], M=cs["M"]) for (b, g) in cores])
    r3 = _run(build_attn(), [dict(qT=r1[i]["o0"], kT=r1[i]["o1"], v=r2[i]["o0"], sg=r2[i]["o1"],
                                  ctok=r2[i]["ctok"], cref=r2[i]["cref"], ident=cs["ident"], mneg=cs["mneg"])
                             for i in range(8)])
    del r1, r2
    ones16 = np.ones((128, NCH), np.float32)
    nc_w = build_wout(False)
    in4 = []
    for b in range(B):
        for hf in range(2):
            ycat = np.concatenate([r3[2 * b + 0]["y"][hf * H2:(hf + 1) * H2], r3[2 * b + 1]["y"][hf * H2:(hf + 1) * H2]],
                                  axis=1)
            in4.append(dict(y=np.ascontiguousarray(ycat), x=np.ascontiguousarray(x[b, hf * H2:(hf + 1) * H2]),
                            w=w_out[0], rg=ones16, ident=cs["ident"]))
    r4 = _run(nc_w, in4)
    del r3, in4
    x1 = [np.concatenate([r4[2 * b]["out"], r4[2 * b + 1]["out"]], axis=0) for b in range(B)]
    del r4

    g1 = _t16(norm_gains[1])
    r5 = _run(build_proj_fm(), [dict(x=x1[b], gain_t=g1, ident=cs["ident"], w0=wcol(hg_w, 0, g),
                                     w1=wcol(hg_w, W, g)) for (b, g) in cores])
    r6 = _run(build_proj_tm(False), [dict(x=x1[b], gain_t=g1, ident=cs["ident"], w0=wcol(hg_w, 2 * W, g),
                                          w1=wcol(hg_w, 3 * W, g)) for (b, g) in cores])

    def lbl_t(g):
        sl = lbl[:, g * WC:(g + 1) * WC]
        return np.ascontiguousarray(sl.reshape(2, HPC, 128).transpose(2, 0, 1).reshape(128, 2 * HPC))

    r7 = _run(build_hgrn(), [dict(qrT=r5[i]["o0"], frT=r5[i]["o1"], v=r6[i]["o0"], sg=r6[i]["o1"],
                                  lbl=lbl_t(cores[i][1]), ident=cs["ident"], cmask=cs["cmask"], segm=cs["segm"])
                             for i in range(8)])
    del r5, r6
    in8 = []
    for b in range(B):
        for hf in range(2):
            ycat = np.concatenate([r7[2 * b + 0]["y"][hf * H2:(hf + 1) * H2], r7[2 * b + 1]["y"][hf * H2:(hf + 1) * H2]],
                                  axis=1)
            in8.append(dict(y=np.ascontiguousarray(ycat), x=np.ascontiguousarray(x1[b][hf * H2:(hf + 1) * H2]),
                            w=w_out[1], rg=_t16(onorm), ident=cs["ident"], fg=final_gain))
    r8 = _run(build_wout(True), in8)
    out = np.stack([np.concatenate([r8[2 * b]["out"], r8[2 * b + 1]["out"]], axis=0) for b in range(B)], axis=0)
    return out.astype(np.float32)
```
